# Optimizing a Trainium2 kernel written in Bass

```python
import math
import jax, jax.numpy as jnp
from jax import lax
import numpy as np

D_MODEL = 1024
BATCH = 8
SEQ = 4096
DEPTH = 1

CHUNK = 64
Q_BLOCK = 128
D_MIX = D_MODEL
ATTN_HEADS = 8
HEAD_DIM = 64
D_ATTN = ATTN_HEADS * HEAD_DIM
D_SSM = D_MIX - D_ATTN
SSM_GROUP = 16
SSM_GROUPS = D_SSM // SSM_GROUP
SSM_STATE = 64
D_FF = -(-(8 * D_MODEL) // (3 * 256)) * 256
D_IN = 3 * D_ATTN + D_SSM
EPS = 1e-6
DT_MIN = 1e-3
DT_MAX = 1e-1

kernel_name = "hymba_stickbreaking_s5_block"


def rmsnorm(x, g):
    xf = x.astype(jnp.float32)
    xf = xf * lax.rsqrt(jnp.mean(xf * xf, axis=-1, keepdims=True) + EPS)
    return xf.astype(x.dtype) * g


def stick_breaking_attention(q, k, v):
    seq_len = q.shape[2]
    scale = HEAD_DIM ** -0.5
    outs = []
    for blk in range(seq_len // Q_BLOCK):
        q0 = blk * Q_BLOCK
        kv_len = q0 + Q_BLOCK
        qb = q[:, :, q0:kv_len]
        kb = k[:, :, :kv_len]
        vb = v[:, :, :kv_len]
        z = jnp.einsum('bhqd,bhkd->bhqk', qb, kb).astype(jnp.float32) * scale
        t_pos = q0 + jnp.arange(Q_BLOCK)[:, None]
        s_pos = jnp.arange(kv_len)[None, :]
        before = s_pos < t_pos
        log1m = jnp.where(before, -jax.nn.softplus(z), 0.0)
        rc = lax.cumsum(log1m, axis=3, reverse=True)
        log_w = jax.nn.log_sigmoid(z) + (rc - log1m)
        w = jnp.where(before, jnp.exp(log_w), 0.0)
        outs.append(jnp.einsum('bhqk,bhkd->bhqd', w.astype(vb.dtype), vb))
    return jnp.concatenate(outs, axis=2)


def _ssm_combine(a, b):
    a_lr, a_li, a_xr, a_xi = a
    b_lr, b_li, b_xr, b_xi = b
    lr = a_lr * b_lr - a_li * b_li
    li = a_lr * b_li + a_li * b_lr
    xr = b_lr * a_xr - b_li * a_xi + b_xr
    xi = b_lr * a_xi + b_li * a_xr + b_xi
    return (lr, li, xr, xi)


def s5_ssm(u, lambda_re, lambda_im, log_step, b_re, b_im, c_re, c_im, d_skip):
    bsz, seq_len, _ = u.shape
    ug = u.reshape(bsz, seq_len, SSM_GROUPS, SSM_GROUP).astype(jnp.float32)
    lam_re = lambda_re.astype(jnp.float32)
    lam_im = lambda_im.astype(jnp.float32)
    dt = jnp.exp(log_step.astype(jnp.float32))[:, None]
    mag = jnp.exp(lam_re * dt)
    ang = lam_im * dt
    lb_re = mag * jnp.cos(ang)
    lb_im = mag * jnp.sin(ang)
    den = lam_re * lam_re + lam_im * lam_im
    num_re = lb_re - 1.0
    f_re = (num_re * lam_re + lb_im * lam_im) / den
    f_im = (lb_im * lam_re - num_re * lam_im) / den
    br = b_re.astype(jnp.float32)
    bi = b_im.astype(jnp.float32)
    bb_re = f_re[..., None] * br - f_im[..., None] * bi
    bb_im = f_re[..., None] * bi + f_im[..., None] * br
    bu_re = jnp.einsum('blgh,gph->blgp', ug, bb_re)
    bu_im = jnp.einsum('blgh,gph->blgp', ug, bb_im)
    a_re = jnp.broadcast_to(lb_re, bu_re.shape)
    a_im = jnp.broadcast_to(lb_im, bu_im.shape)
    _, _, x_re, x_im = lax.associative_scan(_ssm_combine, (a_re, a_im, bu_re, bu_im), axis=1)
    y = (jnp.einsum('blgp,ghp->blgh', x_re, c_re.astype(jnp.float32))
         - jnp.einsum('blgp,ghp->blgh', x_im, c_im.astype(jnp.float32))
         + d_skip.astype(jnp.float32) * ug)
    return y.reshape(bsz, seq_len, D_SSM).astype(u.dtype)


def setup_inputs(seed: int = 0) -> dict:
    key = jax.random.key(seed)
    ks = jax.random.split(key, 24)
    f32 = jnp.float32
    G, P, H = SSM_GROUPS, SSM_STATE, SSM_GROUP
    x = jax.random.normal(ks[0], (BATCH, SEQ, D_MODEL), f32)
    norm1_g = 1.0 + 0.02 * jax.random.normal(ks[1], (DEPTH, D_MODEL), f32)
    w_in = jax.random.normal(ks[2], (DEPTH, D_MODEL, D_IN), f32) * D_MODEL ** -0.5
    attn_norm_g = 1.0 + 0.02 * jax.random.normal(ks[3], (DEPTH, D_ATTN), f32)
    lambda_re = -0.5 + 0.01 * jax.random.normal(ks[4], (DEPTH, G, P), f32)
    lambda_im = (math.pi * jnp.arange(P, dtype=f32))[None, None, :] + 0.01 * jax.random.normal(ks[5], (DEPTH, G, P), f32)
    log_step = jax.random.uniform(ks[6], (DEPTH, G), f32, math.log(DT_MIN), math.log(DT_MAX))
    b_re = jax.random.normal(ks[7], (DEPTH, G, P, H), f32) * (2.0 * H) ** -0.5
    b_im = jax.random.normal(ks[8], (DEPTH, G, P, H), f32) * (2.0 * H) ** -0.5
    c_re = jax.random.normal(ks[9], (DEPTH, G, H, P), f32) * (2.0 * P) ** -0.5
    c_im = jax.random.normal(ks[10], (DEPTH, G, H, P), f32) * (2.0 * P) ** -0.5
    d_skip = jax.random.normal(ks[11], (DEPTH, G, H), f32)
    w_glu = jax.random.normal(ks[12], (DEPTH, D_SSM, D_SSM), f32) * D_SSM ** -0.5
    ssm_norm_g = 1.0 + 0.02 * jax.random.normal(ks[13], (DEPTH, D_SSM), f32)
    w_out = jax.random.normal(ks[14], (DEPTH, D_MIX, D_MODEL), f32) * D_MIX ** -0.5
    norm2_g = 1.0 + 0.02 * jax.random.normal(ks[15], (DEPTH, D_MODEL), f32)
    w_gate = jax.random.normal(ks[16], (DEPTH, D_MODEL, D_FF), f32) * D_MODEL ** -0.5
    w_up = jax.random.normal(ks[17], (DEPTH, D_MODEL, D_FF), f32) * D_MODEL ** -0.5
    w_down = jax.random.normal(ks[18], (DEPTH, D_FF, D_MODEL), f32) * D_FF ** -0.5
    final_norm_g = 1.0 + 0.02 * jax.random.normal(ks[19], (D_MODEL,), f32)
    return {"x": x, "norm1_g": norm1_g, "w_in": w_in, "attn_norm_g": attn_norm_g,
            "lambda_re": lambda_re, "lambda_im": lambda_im, "log_step": log_step,
            "b_re": b_re, "b_im": b_im, "c_re": c_re, "c_im": c_im, "d_skip": d_skip,
            "w_glu": w_glu, "ssm_norm_g": ssm_norm_g, "w_out": w_out, "norm2_g": norm2_g,
            "w_gate": w_gate, "w_up": w_up, "w_down": w_down, "final_norm_g": final_norm_g}


def reference(x, norm1_g, w_in, attn_norm_g, lambda_re, lambda_im, log_step, b_re, b_im,
              c_re, c_im, d_skip, w_glu, ssm_norm_g, w_out, norm2_g, w_gate, w_up, w_down,
              final_norm_g):
    bsz, seq_len, _ = x.shape
    for i in range(DEPTH):
        h = rmsnorm(x, norm1_g[i])
        proj = h @ w_in[i]
        q = proj[..., :D_ATTN]
        k = proj[..., D_ATTN:2 * D_ATTN]
        v = proj[..., 2 * D_ATTN:3 * D_ATTN]
        u = proj[..., 3 * D_ATTN:]
        to_heads = lambda t: t.reshape(bsz, seq_len, ATTN_HEADS, HEAD_DIM).transpose(0, 2, 1, 3)
        o_attn = stick_breaking_attention(to_heads(q), to_heads(k), to_heads(v))
        o_attn = o_attn.transpose(0, 2, 1, 3).reshape(bsz, seq_len, D_ATTN)
        o_attn = rmsnorm(o_attn, attn_norm_g[i])
        y = s5_ssm(u, lambda_re[i], lambda_im[i], log_step[i], b_re[i], b_im[i],
                   c_re[i], c_im[i], d_skip[i])
        y = jax.nn.gelu(y)
        y = y * jax.nn.sigmoid(y @ w_glu[i])
        o_ssm = rmsnorm(y, ssm_norm_g[i])
        x = x + jnp.concatenate([o_attn, o_ssm], axis=-1) @ w_out[i]
        h = rmsnorm(x, norm2_g[i])
        x = x + (jax.nn.silu(h @ w_gate[i]) * (h @ w_up[i])) @ w_down[i]
    return rmsnorm(x, final_norm_g)
```

```python
import math
from contextlib import ExitStack
import numpy as np
import concourse.bass as bass
import concourse.mybir as mybir
from concourse.bass_utils import run_bass_kernel_spmd

F32 = mybir.dt.float32
BF16 = mybir.dt.bfloat16
AF = mybir.ActivationFunctionType
ALU = mybir.AluOpType
AX = mybir.AxisListType

D = 1024
DFF = 2816
EPS = 1e-6


class Tok:
    __slots__ = ("w", "r")

    def __init__(self):
        self.w = None
        self.r = []


class Eng:
    def __init__(self, name, h, sem):
        self.name, self.h, self.sem, self.n, self.seen = name, h, sem, 0, {}


class Prog:
    def __init__(self):
        self.nc = bass.Bass("TRN2", target_bir_lowering=False)
        self.es = ExitStack()
        nc = self.nc
        self.E = {}
        for name, h in (("pe", nc.tensor), ("act", nc.scalar), ("dve", nc.vector), ("pool", nc.gpsimd), ("sp", nc.sync)):
            self.E[name] = Eng(name, h, self.es.enter_context(nc.semaphore("s_" + name)))
        self.dsems = []
        self.ndma = 0

    def sb(self, name, shape, dt):
        return self.es.enter_context(self.nc.sbuf_tensor(name, shape, dt))

    def ps(self, name, shape, dt):
        return self.es.enter_context(self.nc.psum_tensor(name, shape, dt))

    def _waits(self, e, R, W):
        deps = {}

        def add(ev, kind):
            if ev is None:
                return
            sem, val, src = ev
            if src == e.name:
                if e.name in ("pe", "sp"):
                    return
            k = id(sem)
            if e.seen.get(k, 0) >= val:
                return
            if k not in deps or deps[k][1] < val:
                deps[k] = (sem, val)

        for t in R:
            add(t.w, "raw")
        for t in W:
            add(t.w, "waw")
            for ev in t.r:
                add(ev, "war")
        for k, (sem, val) in deps.items():
            e.h.wait_ge(sem, val)
            e.seen[k] = val

    def op(self, eng, fn, R=(), W=(), inc=True):
        e = self.E[eng]
        self._waits(e, R, W)
        ins = fn(e.h)
        if inc:
            e.n += 1
            ins.then_inc(e.sem, 1)
            ev = (e.sem, e.n, e.name)
        else:
            ev = (e.sem, e.n + 1, e.name)
        for t in R:
            t.r.append(ev)
        for t in W:
            t.w = ev
            t.r = []
        return ins

    def dma(self, q, out, in_, R=(), W=(), **kw):
        e = self.E[q]
        self._waits(e, R, W)
        key = W[0] if W else R[0]
        if not hasattr(self, "_dsem"):
            self._dsem = {}
        if id(key) not in self._dsem:
            self._dsem[id(key)] = [self.es.enter_context(self.nc.semaphore("d%d" % len(self._dsem))), 0, key]
        rec = self._dsem[id(key)]
        rec[1] += 16
        e.h.dma_start(out=out, in_=in_, **kw).then_inc(rec[0], 16)
        ev = (rec[0], rec[1], "dma")
        for t in R:
            t.r.append(ev)
        for t in W:
            t.w = ev
            t.r = []
        return ev

    def wait_ev(self, eng, ev):
        e = self.E[eng]
        sem, val, _ = ev
        if e.seen.get(id(sem), 0) < val:
            e.h.wait_ge(sem, val)
            e.seen[id(sem)] = val


class Arena:
    def __init__(self, P, name, nbytes):
        self.t = P.sb(name, [128, nbytes // 4], F32)
        self.n = nbytes // 4
        self.off = 0

    def reset(self):
        self.off = 0

    def get(self, shape, dt):
        esz = 2 if dt == BF16 else 4
        nel = 1
        for s in shape[1:]:
            nel *= s
        n4 = (nel * esz + 3) // 4
        v = self.t[:, self.off:self.off + n4]
        self.off += n4
        assert self.off <= self.n, (self.off, self.n)
        if dt != F32:
            v = v.bitcast(dt)
        if len(shape) == 3:
            v = v.rearrange("p (a b) -> p a b", a=shape[1])
        elif len(shape) == 4:
            v = v.rearrange("p (a b c) -> p a b c", a=shape[1], b=shape[2])
        return v


def barrier(P):
    evs = [(e.sem, e.n, "x") for e in P.E.values() if e.n > 0]
    skip = getattr(P, "nobarrier", ())
    evs += [(rec[0], rec[1], "dma") for rec in getattr(P, "_dsem", {}).values() if not any(rec[2] is s for s in skip)]
    for name in ("pe", "act", "dve", "pool", "sp"):
        for ev in evs:
            if ev[0] is not P.E[name].sem:
                P.wait_ev(name, ev)


def build(L=4096, debug=None):
    P = Prog()
    nc = P.nc
    NT = L // 128
    NQ = L // 512
    x = nc.dram_tensor("x", [L, D], F32, kind="ExternalInput").ap()
    norm1_g = nc.dram_tensor("norm1_g", [1, D], F32, kind="ExternalInput").ap()
    w_in = nc.dram_tensor("w_in", [D, 2048], F32, kind="ExternalInput").ap()
    ident_in = nc.dram_tensor("ident", [128, 128], F32, kind="ExternalInput").ap()
    out = nc.dram_tensor("out", [L, D], F32, kind="ExternalOutput").ap()
    dbg = None
    if debug == "attn":
        dbg = nc.dram_tensor("dbg", [128, 4, L], F32, kind="ExternalOutput").ap()

    identf = P.sb("identf", [128, 128], F32)
    identb = P.sb("identb", [128, 128], BF16)
    tri = P.sb("tri", [128, 128], BF16)
    masku = P.sb("masku", [128, 128], BF16)
    ones = P.sb("ones", [128, 128], BF16)
    t_const = Tok()
    t_id = Tok()
    P.dma("sp", identf[:], ident_in, W=[t_id])
    P.op("dve", lambda v: v.tensor_copy(out=identb[:], in_=identf[:]), R=[t_id], W=[t_const])
    P.op("pool", lambda g: g.memset(ones[:], 1.0), W=[t_const])
    P.op("pool", lambda g: g.affine_select(out=tri[:], in_=ones[:], pattern=[[-1, 128]], compare_op=ALU.is_ge,
                                           fill=0.0, base=0, channel_multiplier=1), R=[t_const], W=[t_const])
    P.op("pool", lambda g: g.affine_select(out=masku[:], in_=ones[:], pattern=[[1, 128]], compare_op=ALU.is_gt,
                                           fill=0.0, base=0, channel_multiplier=-1), R=[t_const], W=[t_const])

    g1b = P.sb("g1b", [128, D], F32)
    t_g1 = Tok()
    P.dma("sp", g1b[:], norm1_g.partition_broadcast(128), W=[t_g1])

    NB = 8
    bank = [P.ps("bank%d" % i, [128, 512], F32) for i in range(NB)]
    t_bank = [Tok() for _ in range(NB)]

    w_gate = nc.dram_tensor("w_gate", [D, DFF], F32, kind="ExternalInput").ap()
    w_up = nc.dram_tensor("w_up", [D, DFF], F32, kind="ExternalInput").ap()
    w_down = nc.dram_tensor("w_down", [DFF, D], F32, kind="ExternalInput").ap()
    NF = DFF // 128
    wg_s = nc.dram_tensor("wg_s", [NF, 128, D], BF16).ap()
    wu_s = nc.dram_tensor("wu_s", [NF, 128, D], BF16).ap()
    wd_s = nc.dram_tensor("wd_s", [DFF, D], BF16).ap()
    t_wgs, t_wus, t_wds = Tok(), Tok(), Tok()

    def weight_cast_units():
        units = []
        for f in range(NF):
            for (src, dst, tk) in ((w_gate, wg_s, t_wgs), (w_up, wu_s, t_wus)):
                units.append(lambda f=f, src=src, dst=dst, tk=tk: P.dma(
                    "pool", dst[f].rearrange("p (c n) -> p c n", c=8), src.rearrange("(c p) n -> p c n", p=128)[:, :, 128 * f:128 * f + 128], W=[tk]))
            units.append(lambda f=f: P.dma("pool", wd_s[128 * f:128 * f + 128, :], w_down[128 * f:128 * f + 128, :], W=[t_wds]))
        return units

    XN = Arena(P, "XN", 64 * 1024)
    WK = Arena(P, "WK", 72 * 1024)
    xnT = XN.get([128, 8, L], BF16)

    def SBW(name, shape, dt):
        return WK.get(shape, dt)
    t_xnT = Tok()
    catT = P.sb("catT", [128, 8, L], BF16)
    t_cat = [Tok() for _ in range(8)]

    NXB = 4
    xin = [SBW("xin%d" % i, [128, D], F32) for i in range(NXB)]
    t_xin = [Tok() for _ in range(NXB)]
    junk = SBW("junk", [128, D], BF16)
    t_junk = Tok()
    ss = P.sb("ss", [128, NT], F32)
    rs = P.sb("rs", [128, NT], F32)
    t_ss = [Tok() for _ in range(NT)]
    xnb = [SBW("xnb%d" % i, [128, D], BF16) for i in range(2)]
    t_xnb = [Tok() for _ in range(2)]
    def pa_a(i):
        xb_ = i % NXB
        P.dma("sp", xin[xb_][:], x[128 * i:128 * i + 128, :], W=[t_xin[xb_]])
        P.op("act", lambda a: a.activation(out=junk[:], in_=xin[xb_][:], func=AF.Square, accum_out=ss[:, i:i + 1]),
             R=[t_xin[xb_]], W=[t_junk, t_ss[i]])

    def pa_b(i):
        P.op("dve", lambda v: v.tensor_scalar(out=rs[:, i:i + 1], in0=ss[:, i:i + 1], scalar1=1.0 / D, scalar2=EPS, op0=ALU.mult, op1=ALU.add),
             R=[t_ss[i]], W=[t_ss[i]])
        P.op("act", lambda a: a.activation(out=rs[:, i:i + 1], in_=rs[:, i:i + 1], func=AF.Sqrt), R=[t_ss[i]], W=[t_ss[i]])

    def pa_c(i):
        b, xb_ = i % 2, i % NXB
        P.op("dve", lambda v: v.reciprocal(out=rs[:, i:i + 1], in_=rs[:, i:i + 1]), R=[t_ss[i]], W=[t_ss[i]])
        P.op("dve", lambda v: v.scalar_tensor_tensor(out=xnb[b][:], in0=xin[xb_][:], scalar=rs[:, i:i + 1], in1=g1b[:], op0=ALU.mult, op1=ALU.mult),
             R=[t_xin[xb_], t_ss[i], t_g1], W=[t_xnb[b]])
        pb = i % 2
        pT = bank[pb][:].bitcast(BF16)
        for j in range(8):
            P.op("pe", lambda t, j=j: t.transpose(out=pT[:, 128 * j:128 * j + 128], in_=xnb[b][:, 128 * j:128 * j + 128], identity=identb[:]),
                 R=[t_xnb[b], t_const], W=[t_bank[pb]], inc=(j == 7))

    def pa_d(i):
        pb = i % 2
        pT = bank[pb][:].bitcast(BF16)
        P.op("dve", lambda v: v.tensor_copy(out=xnT[:, :, 128 * i:128 * i + 128], in_=pT.rearrange("p (j t) -> p j t", j=8)),
             R=[t_bank[pb]], W=[t_xnT])

    for it in range(NT + 3):
        if it < NT:
            pa_a(it)
        if 0 <= it - 1 < NT:
            pa_b(it - 1)
        if 0 <= it - 2 < NT:
            pa_c(it - 2)
        if 0 <= it - 3 < NT:
            pa_d(it - 3)

    lam_re = nc.dram_tensor("lambda_re", [32, 64], F32, kind="ExternalInput").ap()
    lam_im = nc.dram_tensor("lambda_im", [32, 64], F32, kind="ExternalInput").ap()
    log_step = nc.dram_tensor("log_step", [1, 32], F32, kind="ExternalInput").ap()
    b_re = nc.dram_tensor("b_re", [32, 64, 16], F32, kind="ExternalInput").ap()
    b_im = nc.dram_tensor("b_im", [32, 64, 16], F32, kind="ExternalInput").ap()
    d_skip = nc.dram_tensor("d_skip", [32, 16], F32, kind="ExternalInput").ap()
    PSTG = catT[:, 6:8, :].rearrange("p a b -> p (a b)").bitcast(F32)
    LRE, LIM, LS = PSTG[:, 0:16], PSTG[:, 16:32], PSTG[:, 32:48]
    DCOL = PSTG[:, 48:80]
    BRE = PSTG[:, 128:384].rearrange("p (q h) -> p q h", q=16)
    BIM = PSTG[:, 384:640].rearrange("p (q h) -> p q h", q=16)
    t_par = Tok()
    P.nobarrier = [t_par]
    for half in range(2):
        hs = slice(64 * half, 64 * half + 64)
        P.dma("sp", LRE[hs, :], lam_re.rearrange("(q two) p -> two p q", two=2)[half], W=[t_par], allow_slow_non_contiguous=True)
        P.dma("sp", LIM[hs, :], lam_im.rearrange("(q two) p -> two p q", two=2)[half], W=[t_par], allow_slow_non_contiguous=True)
        P.dma("sp", LS[hs, :], log_step.rearrange("o (q two) -> o two q", two=2)[:, half, :].partition_broadcast(64), W=[t_par],
              allow_slow_non_contiguous=True)
        P.dma("sp", BRE[hs], b_re.rearrange("(q two) p h -> two p q h", two=2)[half], W=[t_par])
        P.dma("sp", BIM[hs], b_im.rearrange("(q two) p h -> two p q h", two=2)[half], W=[t_par])
    for s8 in range(8):
        P.dma("sp", DCOL[16 * s8:16 * s8 + 16, :], d_skip.rearrange("g h -> h g"), W=[t_par], allow_slow_non_contiguous=True)
    c_re = nc.dram_tensor("c_re", [512, 64], F32, kind="ExternalInput").ap()
    c_im = nc.dram_tensor("c_im", [512, 64], F32, kind="ExternalInput").ap()
    WU4 = None
    t_wu4 = Tok()
    if L >= 4096:
        WU4 = PSTG[:, 2048:4096].bitcast(BF16).rearrange("p (r c n) -> p r c n", r=4, c=8)
        for r in range(4):
            P.dma("pool", WU4[:, r], w_in.rearrange("(c p) n -> p c n", p=128)[:, :, 1536 + 128 * r:1536 + 128 * r + 128], W=[t_wu4])
        P.nobarrier.append(t_wu4)
    CST8 = None
    if L >= 2048:
        CST8 = PSTG[:, 1024:2048].rearrange("p (k c) -> p k c", k=8)
        for ci, csrc in enumerate((c_re, c_im)):
            for j in range(4):
                P.dma("sp", CST8[:, 4 * ci + j, 0:64], csrc[128 * j:128 * j + 128, :], W=[t_par])
                P.dma("sp", CST8[:, 4 * ci + j, 64:128], csrc[128 * j:128 * j + 128, :], W=[t_par])

    barrier(P)
    WK.reset()
    wq = SBW("wq", [128, 8, 128], BF16)
    wk = SBW("wk", [128, 8, 128], BF16)
    wv = SBW("wv", [128, 8, 128], BF16)
    t_wq, t_wk, t_wv = Tok(), Tok(), Tok()
    qz = [SBW("qz%d" % h, [128, L], BF16) for h in range(2)]
    t_qz = [[Tok() for _ in range(NQ)] for _ in range(2)]
    kT = SBW("kT", [128, L], BF16)
    t_kT = [Tok() for _ in range(NQ)]
    vtm = SBW("vtm", [128, NT, 128], BF16)
    t_v = [Tok() for _ in range(NQ)]
    NWB = 5
    ubuf = [SBW("ubuf%d" % i, [128, 512], F32) for i in range(NWB)]
    pbf = [SBW("pbf%d" % i, [128, 512], BF16) for i in range(NWB)]
    dV = SBW("dV", [128, NT, 128], BF16)
    t_dv = [Tok() for _ in range(NQ)]
    wtb = [SBW("wtb%d" % i, [128, 512], BF16) for i in range(NWB)]
    t_u = [Tok() for _ in range(NWB)]
    t_w = [Tok() for _ in range(NWB)]
    t_wt = [Tok() for _ in range(NWB)]
    zero512 = SBW("zero512", [128, 512], F32)
    onecol = SBW("onecol", [128, 1], F32)
    t_zc = Tok()
    P.op("pool", lambda g: g.memset(zero512[:], 0.0), W=[t_zc])
    P.op("pool", lambda g: g.memset(onecol[:], 1.0), W=[t_zc])
    onesf = SBW("onesf", [128, 128], F32)
    mlow = SBW("mlow", [128, 128], F32)
    mup = SBW("mup", [128, 128], F32)
    P.op("pool", lambda g: g.memset(onesf[:], 1.0), W=[t_zc])
    P.op("pool", lambda g: g.affine_select(out=mlow[:], in_=onesf[:], pattern=[[-1, 128]], compare_op=ALU.is_ge,
                                           fill=0.0, base=-1, channel_multiplier=1), R=[t_zc], W=[t_zc])
    P.op("pool", lambda g: g.affine_select(out=mup[:], in_=onesf[:], pattern=[[1, 128]], compare_op=ALU.is_ge,
                                           fill=0.0, base=0, channel_multiplier=-1), R=[t_zc], W=[t_zc])
    negm = SBW("negm", [128, 128], BF16)
    P.op("pool", lambda g: g.tensor_scalar_mul(out=negm[:], in0=mup[:], scalar1=-30000.0), R=[t_zc], W=[t_zc])
    shm = SBW("shm", [128, 128], BF16)
    dmm = SBW("dmm", [128, 128], BF16)
    corn = SBW("corn", [128, 128], BF16)
    P.op("pool", lambda g: g.affine_select(out=shm[:], in_=ones[:], pattern=[[1, 128]], compare_op=ALU.is_equal,
                                           fill=0.0, base=-1, channel_multiplier=-1), R=[t_const], W=[t_zc])
    P.op("pool", lambda g: g.tensor_tensor(out=dmm[:], in0=shm[:], in1=identb[:], op=ALU.subtract), R=[t_zc, t_const], W=[t_zc])
    P.op("pool", lambda g: g.affine_select(out=corn[:], in_=ones[:], pattern=[[-1, 128]], compare_op=ALU.is_equal,
                                           fill=0.0, base=-127, channel_multiplier=1), R=[t_const], W=[t_zc])
    for h in range(2):
        P.op("pool", lambda g, h=h: g.memset(qz[h][:], 0.0), W=t_qz[h])
    ZB = [0, 1, 2]
    TBK = [3, 4, 5]
    nchunk = 0
    nob = 0
    wview = w_in.rearrange("(c p) n -> p c n", p=128)

    def load_proj_weights(hp_):
        P.dma("pool", wq[:], wview[:, :, 128 * hp_:128 * hp_ + 128], W=[t_wq])
        P.dma("pool", wk[:], wview[:, :, 512 + 128 * hp_:512 + 128 * hp_ + 128], W=[t_wk])
        P.dma("pool", wv[:], wview[:, :, 1024 + 128 * hp_:1024 + 128 * hp_ + 128], W=[t_wv])
    load_proj_weights(0)
    pjn = [0]

    def proj_units(g4):
        tsl = slice(512 * g4, 512 * g4 + 512)

        def nb():
            pjn[0] += 1
            return pjn[0] % 6
        banks = dict(v=None)

        def u_q():
            pb = nb()
            for c in range(8):
                P.op("pe", lambda t, c=c: t.matmul(bank[pb][:], lhsT=wq[:, c, :], rhs=xnT[:, c, tsl], start=(c == 0), stop=(c == 7)),
                     R=[t_wq, t_xnT], W=[t_bank[pb]], inc=(c == 7))
            for h in range(2):
                hs = slice(64 * h, 64 * h + 64)
                P.op("act", lambda a, h=h, hs=hs: a.activation(out=qz[h][hs, tsl], in_=bank[pb][hs, :], func=AF.Copy, scale=0.125),
                     R=[t_bank[pb]], W=[t_qz[h][g4]])

        def u_k():
            pb2 = nb()
            for c in range(8):
                P.op("pe", lambda t, c=c: t.matmul(bank[pb2][:], lhsT=wk[:, c, :], rhs=xnT[:, c, tsl], start=(c == 0), stop=(c == 7)),
                     R=[t_wk, t_xnT], W=[t_bank[pb2]], inc=(c == 7))
            P.op("act", lambda a: a.copy(out=kT[:, tsl], in_=bank[pb2][:]), R=[t_bank[pb2]], W=[t_kT[g4]])

        def mk_v(u):
            def f():
                pb3 = nb()
                i = 4 * g4 + u
                for c in range(8):
                    P.op("pe", lambda t, c=c: t.matmul(bank[pb3][:, 0:128], lhsT=xnT[:, c, 128 * i:128 * i + 128], rhs=wv[:, c, :],
                                                       start=(c == 0), stop=(c == 7)),
                         R=[t_wv, t_xnT], W=[t_bank[pb3]], inc=(c == 7))
                P.op("act", lambda a: a.copy(out=vtm[:, i, :], in_=bank[pb3][:, 0:128]), R=[t_bank[pb3]], W=[t_v[g4]])
            return f

        def u_dv():
            pb4 = nb()
            rv = [t_v[g4]] + ([t_v[g4 - 1]] if g4 > 0 else [])
            for u in range(4):
                i = 4 * g4 + u
                P.op("pe", lambda t, i=i, u=u: t.matmul(bank[pb4][:, 128 * u:128 * u + 128], lhsT=dmm[:], rhs=vtm[:, i, :], start=True, stop=(i == 0)),
                     R=rv + [t_zc], W=[t_bank[pb4]], inc=(i == 0 and u == 3))
                if i > 0:
                    P.op("pe", lambda t, i=i, u=u: t.matmul(bank[pb4][:, 128 * u:128 * u + 128], lhsT=corn[:], rhs=vtm[:, i - 1, :], start=False, stop=True),
                         R=rv + [t_zc], W=[t_bank[pb4]], inc=(u == 3))
            P.op("act", lambda a: a.copy(out=dV[:, 4 * g4:4 * g4 + 4, :], in_=bank[pb4][:].rearrange("p (u n) -> p u n", u=4)),
                 R=[t_bank[pb4]], W=[t_dv[g4]])
        return [u_q, u_k, mk_v(0), mk_v(1), mk_v(2), mk_v(3), u_dv]

    for hp in range(4):
        extra = weight_cast_units() if hp == 0 else []
        chunks = []
        percol = {0: [], 1: []}
        for h in range(2):
            for T in range(NT):
                if T % 4 == 0:
                    ob = 6 + h
                hi = T
                prev = None
                while hi >= 0:
                    lo = max(0, hi - 3)
                    d = dict(h=h, T=T, lo=lo, hi=hi, n=hi - lo + 1, N=128 * (hi - lo + 1), first=(hi == T), last=(lo == 0),
                             ob=ob, prev=prev, hp=hp, evac=(lo == 0 and (T % 4 == 3 or T == NT - 1)))
                    percol[h].append(d)
                    prev = d
                    hi = lo - 1
        for d0, d1 in zip(percol[0], percol[1]):
            for d in (d0, d1):
                d["b"] = nchunk % NWB
                d["zb"] = ZB[nchunk % 3]
                d["tbk"] = TBK[nchunk % 3]
                nchunk += 1
                chunks.append(d)

        def S1(d):
            b, N, h, T, zb = d["b"], d["N"], d["h"], d["T"], d["zb"]
            P.op("pe", lambda t: t.matmul(bank[zb][:, :N], lhsT=qz[h][:, 128 * T:128 * T + 128], rhs=kT[:, 128 * d["lo"]:128 * (d["hi"] + 1)],
                                          start=True, stop=not d["first"]), R=[t_kT[g_] for g_ in range(d["lo"] // 4, d["hi"] // 4 + 1)] + [t_qz[h][T // 4]],
                 W=[t_bank[zb]], inc=not d["first"])
            if d["first"]:
                P.op("pe", lambda t: t.matmul(bank[zb][:, N - 128:N], lhsT=identb[:], rhs=negm[:], start=False, stop=True),
                     R=[t_const, t_zc], W=[t_bank[zb]])
            P.op("act", lambda a: a.activation(out=ubuf[b][:, :N], in_=bank[zb][:, :N], func=AF.Sigmoid, scale=-1.0),
                 R=[t_bank[zb]], W=[t_u[b]])

        def S2(d):
            b, N = d["b"], d["N"]
            pv = d["prev"]
            if pv is None:
                carry, Rc = 1.0, []
            else:
                carry, Rc = pbf[pv["b"]][:, 0:1], [t_w[pv["b"]]]
            rev = slice(N - 1, None, -1)
            P.op("dve", lambda v: v.tensor_tensor_scan(out=pbf[b][:, rev], data0=ubuf[b][:, rev], data1=zero512[:, :N], initial=carry,
                                                       op0=ALU.mult, op1=ALU.add), R=[t_u[b], t_zc] + Rc, W=[t_w[b]])

        def S3(d):
            b, N, n, tbk = d["b"], d["N"], d["n"], d["tbk"]
            pT = bank[tbk][:].bitcast(BF16)
            for j in range(n):
                P.op("pe", lambda t, j=j: t.transpose(out=pT[:, 128 * j:128 * j + 128], in_=pbf[b][:, 128 * j:128 * j + 128], identity=identb[:]),
                     R=[t_w[b], t_const], W=[t_bank[tbk]], inc=(j == n - 1))
            P.op("act", lambda a: a.copy(out=wtb[b][:, :N], in_=pT[:, :N]), R=[t_bank[tbk]], W=[t_wt[b]])
            if d["first"]:
                P.op("pool", lambda g: g.tensor_tensor(out=wtb[b][:, N - 128:N], in0=wtb[b][:, N - 128:N], in1=masku[:], op=ALU.mult),
                     R=[t_wt[b], t_const], W=[t_wt[b]])

        def S4(d):
            b, n, ob, T, h = d["b"], d["n"], d["ob"], d["T"], d["h"]
            osl = slice(128 * (T % 4), 128 * (T % 4) + 128)
            if d["first"]:
                P.op("pe", lambda t: t.matmul(bank[ob][:, osl], lhsT=vtm[:, T, :], rhs=shm[:], start=True, stop=False),
                     R=[t_v[T // 4], t_zc], W=[t_bank[ob]], inc=False)
                if T > 0:
                    P.op("pe", lambda t: t.matmul(bank[ob][:, osl], lhsT=vtm[:, T - 1, :], rhs=corn[:], start=False, stop=False),
                         R=[t_v[(T - 1) // 4], t_zc], W=[t_bank[ob]], inc=False)
            for j in range(n):
                P.op("pe", lambda t, j=j: t.matmul(bank[ob][:, osl], lhsT=dV[:, d["lo"] + j, :], rhs=wtb[b][:, 128 * j:128 * j + 128],
                                                   start=False, stop=(d["last"] and j == n - 1)),
                     R=[t_dv[(d["lo"] + j) // 4], t_wt[b]], W=[t_bank[ob]], inc=(j == n - 1))
            if d["evac"]:
                hs = slice(64 * h, 64 * h + 64)
                g4 = T // 4
                wdt = 128 * (T % 4 + 1)
                P.op("act", lambda a: a.copy(out=catT[hs, d["hp"], 512 * g4:512 * g4 + wdt], in_=bank[ob][hs, 0:wdt]),
                     R=[t_bank[ob]], W=[t_cat[d["hp"]]])

        nch = len(chunks)
        done_g4 = -1
        pend = []
        q_it, q_k = 0, 1
        every = max(1, (nch - 8) // max(1, len(extra)))
        for it in range(nch + 6):
            if it < nch:
                need = chunks[it]["T"] // 4
                while done_g4 < need or (done_g4 == need and pend):
                    if not pend:
                        done_g4 += 1
                        pend.extend(proj_units(done_g4))
                        continue
                    if done_g4 > need:
                        break
                    pend.pop(0)()
                    if not pend and done_g4 == NT // 4 - 1 and hp < 3:
                        load_proj_weights(hp + 1)
                S1(chunks[it])
                if not pend and done_g4 + 1 < NT // 4:
                    done_g4 += 1
                    pend.extend(proj_units(done_g4))
                    q_it = it
                    q_k = max(1, (8 * (need + 1) - 3) // 7)
                elif pend and done_g4 > need and (it - q_it) % q_k == 0:
                    pend.pop(0)()
                    if not pend and done_g4 == NT // 4 - 1 and hp < 3:
                        load_proj_weights(hp + 1)
            if 0 <= it - 1 < nch:
                S2(chunks[it - 1])
            if 0 <= it - 3 < nch:
                S3(chunks[it - 3])
            if 0 <= it - 6 < nch:
                S4(chunks[it - 6])
            if extra and it % every == every - 1:
                extra.pop(0)()
        while extra:
            extra.pop(0)()

    if debug == "attn":
        stg = P.sb("stg", [128, 4, L], F32)
        t_stg = Tok()
        P.op("dve", lambda v: v.tensor_copy(out=stg[:], in_=catT[:, 0:4, :]), R=t_cat[0:4], W=[t_stg])
        ev = P.dma("sp", dbg, stg[:], R=[t_stg])
        P.wait_ev("sp", ev)
        return P

    barrier(P)
    w_glu = nc.dram_tensor("w_glu", [512, 512], F32, kind="ExternalInput").ap()
    NC = L // 8
    I32 = mybir.dt.int32
    WK.reset()
    uD = WK.get([128, 4, 8, L // 8], BF16)
    wu = WK.get([128, 8, 128], BF16)
    SEL = WK.get([128, 64, 128], BF16)
    NE = 26
    EXV = [-(s + 1) for s in range(8)] + [7 - s for s in range(8)] + [t + 1 for t in range(8)] + [64, 512]
    t_u = [Tok() for _ in range(4)]
    t_wu = Tok()
    t_tab = Tok()
    wview = w_in.rearrange("(c p) n -> p c n", p=128)
    def wk(shape, dt=F32):
        return WK.get(shape, dt)
    LR, ANG = wk([128, 16]), wk([128, 16])
    DEN, NUMR, FRE, FIM, TA, TB_ = wk([128, 16]), wk([128, 16]), wk([128, 16]), wk([128, 16]), wk([128, 16]), wk([128, 16])
    EX = wk([128, NE])
    PI_ = wk([128, NE, 16])
    PR_ = wk([128, NE, 16])
    MG = wk([128, NE, 16])
    BBR, BBI = wk([128, 16, 16]), wk([128, 16, 16])
    CRE, CIM = wk([128, 16, 16]), wk([128, 16, 16])
    CST = wk([128, 128])
    MASKB = wk([128, 8, 16])
    ONESF = wk([128, 8, 16])
    T1_OFF = WK.off
    T1, T2, T3, T4 = wk([128, 512]), wk([128, 512]), wk([128, 512]), wk([128, 512])
    for i, n in enumerate(EXV):
        P.op("pool", lambda g, i=i, n=n: g.memset(EX[:, i:i + 1], float(n)), W=[t_tab])
    P.op("pool", lambda g: g.memset(ONESF[:], 1.0), W=[t_tab])
    P.op("pool", lambda g: g.affine_select(out=MASKB[:], in_=ONESF[:], pattern=[[16, 8], [0, 16]], compare_op=ALU.is_ge,
                                           fill=0.0, base=15, channel_multiplier=-1), R=[t_tab], W=[t_tab])
    RW = dict(R=[t_par, t_tab], W=[t_tab])
    TWO_PI = 2.0 * math.pi
    P.op("act", lambda a: a.activation(out=LS[:], in_=LS[:], func=AF.Exp), **RW)
    P.op("dve", lambda v: v.tensor_tensor(out=LR[:], in0=LRE[:], in1=LS[:], op=ALU.mult), **RW)
    P.op("dve", lambda v: v.tensor_tensor(out=ANG[:], in0=LIM[:], in1=LS[:], op=ALU.mult), **RW)
    exb = EX[:].unsqueeze(2).broadcast_to([128, NE, 16])
    P.op("dve", lambda v: v.tensor_tensor(out=PI_[:], in0=exb, in1=ANG[:].unsqueeze(1).broadcast_to([128, NE, 16]), op=ALU.mult), **RW)
    P.op("dve", lambda v: v.tensor_tensor(out=MG[:], in0=exb, in1=LR[:].unsqueeze(1).broadcast_to([128, NE, 16]), op=ALU.mult), **RW)
    P.op("act", lambda a: a.activation(out=MG[:], in_=MG[:], func=AF.Exp), **RW)
    XI = T1[:, 0:NE * 16].bitcast(I32).rearrange("p (a b) -> p a b", a=NE)
    XM = T2[:, 0:NE * 16].rearrange("p (a b) -> p a b", a=NE)
    P.op("dve", lambda v: v.tensor_scalar(out=PI_[:], in0=PI_[:], scalar1=1.0 / TWO_PI, scalar2=32.0, op0=ALU.mult, op1=ALU.add), **RW)
    P.op("dve", lambda v: v.tensor_scalar_add(out=PR_[:], in0=PI_[:], scalar1=0.25), **RW)
    for X in (PI_, PR_):
        P.op("dve", lambda v, X=X: v.tensor_copy(out=XI, in_=X[:]), **RW)
        P.op("dve", lambda v, X=X: v.tensor_tensor(out=X[:], in0=X[:], in1=XI, op=ALU.subtract), **RW)
        P.op("dve", lambda v, X=X: v.tensor_single_scalar(out=XM, in_=X[:], scalar=0.5, op=ALU.is_gt), **RW)
        P.op("dve", lambda v, X=X: v.tensor_tensor(out=X[:], in0=X[:], in1=XM, op=ALU.subtract), **RW)
    P.op("act", lambda a: a.activation(out=PR_[:], in_=PR_[:], func=AF.Sin, scale=TWO_PI), **RW)
    P.op("act", lambda a: a.activation(out=PI_[:], in_=PI_[:], func=AF.Sin, scale=TWO_PI), **RW)
    P.op("dve", lambda v: v.tensor_tensor(out=PR_[:], in0=PR_[:], in1=MG[:], op=ALU.mult), **RW)
    P.op("dve", lambda v: v.tensor_tensor(out=PI_[:], in0=PI_[:], in1=MG[:], op=ALU.mult), **RW)
    lbr, lbi = PR_[:, 16, :], PI_[:, 16, :]

    def tt(out_, a_, b_, op_):
        P.op("dve", lambda v: v.tensor_tensor(out=out_, in0=a_, in1=b_, op=op_), **RW)
    tt(DEN[:], LRE[:], LRE[:], ALU.mult)
    tt(TA[:], LIM[:], LIM[:], ALU.mult)
    tt(DEN[:], DEN[:], TA[:], ALU.add)
    P.op("dve", lambda v: v.reciprocal(out=DEN[:], in_=DEN[:]), **RW)
    P.op("dve", lambda v: v.tensor_scalar_add(out=NUMR[:], in0=lbr, scalar1=-1.0), **RW)
    tt(TA[:], NUMR[:], LRE[:], ALU.mult)
    tt(TB_[:], lbi, LIM[:], ALU.mult)
    tt(TA[:], TA[:], TB_[:], ALU.add)
    tt(FRE[:], TA[:], DEN[:], ALU.mult)
    tt(TA[:], lbi, LRE[:], ALU.mult)
    tt(TB_[:], NUMR[:], LIM[:], ALU.mult)
    tt(TA[:], TA[:], TB_[:], ALU.subtract)
    tt(FIM[:], TA[:], DEN[:], ALU.mult)
    frb = FRE[:].unsqueeze(2).broadcast_to([128, 16, 16])
    fib = FIM[:].unsqueeze(2).broadcast_to([128, 16, 16])
    t1b = T1[:, 0:256].rearrange("p (q h) -> p q h", q=16)
    t2b = T2[:, 0:256].rearrange("p (q h) -> p q h", q=16)
    tt(t1b, frb, BRE[:], ALU.mult)
    tt(t2b, fib, BIM[:], ALU.mult)
    tt(BBR[:], t1b, t2b, ALU.subtract)
    tt(t1b, frb, BIM[:], ALU.mult)
    tt(t2b, fib, BRE[:], ALU.mult)
    tt(BBI[:], t1b, t2b, ALU.add)
    t_cst = Tok()
    for ci, (csrc, CDST) in enumerate(((c_re, CRE), (c_im, CIM))):
        for j in range(4):
            if CST8 is not None:
                cst_ap, Rc_ = CST8[:, 4 * ci + j, :], [t_par]
            else:
                P.dma("sp", CST[:, 0:64], csrc[128 * j:128 * j + 128, :], W=[t_cst])
                P.dma("sp", CST[:, 64:128], csrc[128 * j:128 * j + 128, :], W=[t_cst])
                cst_ap, Rc_ = CST[:], [t_cst]
            cb_ = 4 + (4 * ci + j) % 2
            P.op("pe", lambda t, cst_ap=cst_ap, cb_=cb_: t.transpose(out=bank[cb_][:, 0:128], in_=cst_ap, identity=identf[:]),
                 R=Rc_ + [t_const], W=[t_bank[cb_]])
            for half in range(2):
                hs = slice(64 * half, 64 * half + 64)
                src_v = bank[cb_][hs, 0:128].rearrange("p (q two h) -> p q two h", q=4, two=2)[:, :, half, :]
                P.op("dve", lambda v, hs=hs, j=j, src_v=src_v, CDST=CDST: v.tensor_copy(out=CDST[hs, 4 * j:4 * j + 4, :], in_=src_v),
                     R=[t_bank[cb_]], W=[t_tab])
    RHO, FR = wk([128, 16]), wk([128, 16])
    P.op("dve", lambda v: v.tensor_copy(out=RHO[:], in_=MG[:, 23, :]), **RW)
    P.op("dve", lambda v: v.tensor_scalar(out=FR[:], in0=ANG[:], scalar1=8.0 / TWO_PI, scalar2=32.0, op0=ALU.mult, op1=ALU.add), **RW)
    FRI = TA[:].bitcast(I32)
    P.op("dve", lambda v: v.tensor_copy(out=FRI, in_=FR[:]), **RW)
    P.op("dve", lambda v: v.tensor_tensor(out=FR[:], in0=FR[:], in1=FRI, op=ALU.subtract), **RW)
    for r in range(4):
        if WU4 is not None:
            wu_r, t_wur = WU4[:, r], t_wu4
        else:
            P.dma("pool", wu[:], wview[:, :, 1536 + 128 * r:1536 + 128 * r + 128], W=[t_wu])
            wu_r, t_wur = wu, t_wu
        for tt in range(NQ):
            tsl = slice(512 * tt, 512 * tt + 512)
            pb = tt % 2
            for c in range(8):
                P.op("pe", lambda t, c=c, tsl=tsl, pb=pb, wu_r=wu_r: t.matmul(bank[pb][:], lhsT=wu_r[:, c, :], rhs=xnT[:, c, tsl],
                                                                              start=(c == 0), stop=(c == 7)),
                     R=[t_wur, t_xnT], W=[t_bank[pb]], inc=(c == 7))
            P.op("act", lambda a, r=r, tt=tt, pb=pb: a.copy(out=uD[:, r, :, 64 * tt:64 * tt + 64],
                                                       in_=bank[pb][:].rearrange("p (c s) -> p s c", s=8)), R=[t_bank[pb]], W=[t_u[r]])
    t_sel = Tok()
    P.op("pool", lambda g: g.memset(SEL[:], 0.0), W=[t_sel])
    for k in range(8):
        for s8 in range(8):
            P.op("pool", lambda g, k=k, s8=s8: g.tensor_copy(out=SEL[:, k * 8 + s8, 16 * s8:16 * s8 + 16],
                                                             in_=identb[:, 16 * k:16 * k + 16]), R=[t_const], W=[t_sel])
    barrier(P)
    CIDX = wu[:].rearrange("p a b -> p (a b)").bitcast(F32)[:, 0:NC]
    P.op("pool", lambda g: g.iota(CIDX, pattern=[[1, NC]], base=1, channel_multiplier=0, allow_small_or_imprecise_dtypes=True), W=[t_tab])
    XN.reset()

    def xg(shape, dt=F32):
        return XN.get(shape, dt)
    UF = xg([128, 8, NC], BF16)
    ZRE, ZIM = xg([128, 4, NC]), xg([128, 4, NC])
    S1R, S1I = xg([128, 4, NC], BF16), xg([128, 4, NC], BF16)
    KBNr, KBNi, KB8r, KB8i, QCr, QCni = [xg([128, 4, 128], BF16) for _ in range(6)]
    MZr, MZi = xg([128, 4, 128], BF16), xg([128, 4, 128], BF16)
    M1 = xg([128, 8, 128], BF16)
    QCZr, QCZi = xg([128, 8, 128], BF16), xg([128, 8, 128], BF16)
    YF = ZRE.rearrange("p q c -> p (q c)").bitcast(BF16).rearrange("p (g c) -> p g c", g=8)
    SQSG = xg([128, 2, 512])
    SQ, SG = SQSG[:, 0, :], SQSG[:, 1, :]
    wglu = xg([128, 4, 512], BF16)
    COS, SIN = xg([128, 2, NC]), xg([128, 2, NC])
    t_wglu = Tok()
    P.dma("pool", wglu[:], w_glu.rearrange("(c p) n -> p c n", p=128), W=[t_wglu])
    t_r = Tok()
    t_m1, t_mz, t_uf, t_z, t_s1, t_gl, t_tmp = Tok(), Tok(), Tok(), Tok(), Tok(), Tok(), Tok()
    t_trig = [Tok(), Tok()]
    P.op("pool", lambda g: g.memset(S1R[:, :, 0:1], 0.0), W=[t_s1])
    P.op("pool", lambda g: g.memset(S1I[:, :, 0:1], 0.0), W=[t_s1])
    t_yg = Tok()
    sh4 = [128, 4, 8, 16]
    v4 = lambda ap: ap.rearrange("p (q s h) -> p q s h", q=4, s=8)
    T1v, T2v, T3v, T4v = v4(T1[:]), v4(T2[:]), v4(T3[:]), v4(T4[:])
    RWr = dict(R=[t_r, t_tab], W=[t_r])
    nbk = [0]

    def next_bank():
        nbk[0] = (nbk[0] + 1) % 8
        return nbk[0]

    for r in range(4):
        qs = slice(4 * r, 4 * r + 4)

        def ctable(dre, dim_, negim, e0, Xr, Xi):
            pwr = PR_[:, e0:e0 + 8, qs].rearrange("p s q -> p q s").unsqueeze(3).broadcast_to(sh4)
            pwi = PI_[:, e0:e0 + 8, qs].rearrange("p s q -> p q s").unsqueeze(3).broadcast_to(sh4)
            xr = Xr[:, qs, :].unsqueeze(2).broadcast_to(sh4)
            xi = Xi[:, qs, :].unsqueeze(2).broadcast_to(sh4)
            dre4 = dre[:].rearrange("p q (s h) -> p q s h", s=8)
            dim4 = dim_[:].rearrange("p q (s h) -> p q s h", s=8)
            for (o_, a_, b_, op_) in ((T1v, pwr, xr, ALU.mult), (T2v, pwi, xi, ALU.mult), (dre4, T1v, T2v, ALU.subtract),
                                      (T3v, pwr, xi, ALU.mult), (T4v, pwi, xr, ALU.mult)):
                P.op("dve", lambda v, o_=o_, a_=a_, b_=b_, op_=op_: v.tensor_tensor(out=o_, in0=a_, in1=b_, op=op_), **RWr)
            if negim:
                P.op("dve", lambda v: v.scalar_tensor_tensor(out=dim_[:], in0=T3[:].rearrange("p (q x) -> p q x", q=4), scalar=-1.0,
                                                              in1=T4[:].rearrange("p (q x) -> p q x", q=4),
                                                              op0=ALU.mult, op1=ALU.subtract), **RWr)
            else:
                P.op("dve", lambda v: v.tensor_tensor(out=dim4, in0=T3v, in1=T4v, op=ALU.add), **RWr)
        for gl in range(8):
            mb = next_bank()
            for s8 in range(8):
                P.op("pe", lambda t, mb=mb, gl=gl, s8=s8, r=r: t.matmul(bank[mb][:, :NC], lhsT=SEL[:, gl * 8 + s8, :], rhs=uD[:, r, s8, :],
                                                                      start=(s8 == 0), stop=(s8 == 7)),
                     R=[t_sel, t_u[r]], W=[t_bank[mb]], inc=(s8 == 7))
            P.op("act", lambda a, mb=mb, gl=gl: a.copy(out=UF[:, gl, :], in_=bank[mb][:, :NC]), R=[t_bank[mb]], W=[t_uf])
        ctable(KB8r, KB8i, False, 8, BBR, BBI)
        for pr in range(4):
            for (srcT, dstT) in ((KB8r, MZr), (KB8i, MZi)):
                mb = next_bank()
                pTb = bank[mb][:].bitcast(BF16)
                P.op("pe", lambda t, pTb=pTb, srcT=srcT, pr=pr: t.transpose(out=pTb[:, 0:128], in_=srcT[:, pr, :], identity=identb[:]),
                     R=[t_r, t_const], W=[t_bank[mb]])
                P.op("act", lambda a, pTb=pTb, dstT=dstT, pr=pr: a.copy(out=dstT[:, pr, :], in_=pTb[:, 0:128]), R=[t_bank[mb]], W=[t_mz])
        for pr in range(4):
            for (MZ, ZD) in ((MZr, ZRE), (MZi, ZIM)):
                mb = next_bank()
                for half in range(2):
                    hs = slice(64 * half, 64 * half + 64)
                    P.op("pe", lambda t, mb=mb, hs=hs, MZ=MZ, pr=pr, half=half: t.matmul(bank[mb][hs, :NC], lhsT=MZ[:, pr, hs], rhs=UF[:, 2 * pr + half, :],
                                                                                      start=True, stop=True),
                         R=[t_mz, t_uf], W=[t_bank[mb]], inc=(half == 1))
                P.op("act", lambda a, mb=mb, ZD=ZD, pr=pr: a.copy(out=ZD[:, pr, :], in_=bank[mb][:, :NC]), R=[t_bank[mb]], W=[t_z])
        ctable(KBNr, KBNi, False, 0, BBR, BBI)
        ctable(QCr, QCni, True, 16, CRE, CIM)
        P.op("pool", lambda g: g.memset(QCZr[:], 0.0), W=[t_m1])
        P.op("pool", lambda g: g.memset(QCZi[:], 0.0), W=[t_m1])
        for half in range(2):
            hs = slice(64 * half, 64 * half + 64)
            P.op("pool", lambda g, hs=hs, half=half: g.tensor_copy(out=QCZr[hs].rearrange("p (q two) x -> p q two x", two=2)[:, :, half, :], in_=QCr[hs]),
                 R=[t_r], W=[t_m1])
            P.op("pool", lambda g, hs=hs, half=half: g.tensor_copy(out=QCZi[hs].rearrange("p (q two) x -> p q two x", two=2)[:, :, half, :], in_=QCni[hs]),
                 R=[t_r], W=[t_m1])
        for gl in range(8):
            g = 8 * r + gl
            half, pr = gl % 2, gl // 2
            hs = slice(64 * half, 64 * half + 64)
            mb = next_bank()
            P.op("pe", lambda t, mb=mb, hs=hs, pr=pr: t.matmul(bank[mb][:, 0:128], lhsT=KBNr[hs, pr, :], rhs=QCr[hs, pr, :], start=True, stop=False),
                 R=[t_r], W=[t_bank[mb]], inc=False)
            P.op("pe", lambda t, mb=mb, hs=hs, pr=pr: t.matmul(bank[mb][:, 0:128], lhsT=KBNi[hs, pr, :], rhs=QCni[hs, pr, :], start=False, stop=True),
                 R=[t_r], W=[t_bank[mb]])
            P.op("dve", lambda v, mb=mb: v.tensor_tensor(out=T1[:, 0:128], in0=bank[mb][:, 0:128], in1=MASKB[:].rearrange("p a b -> p (a b)"), op=ALU.mult),
                 R=[t_bank[mb], t_tab, t_r], W=[t_r])
            P.op("dve", lambda v, gl=gl, g=g, half=half: v.scalar_tensor_tensor(
                out=M1[:, gl, :], in0=identf[:], scalar=DCOL[:, g:g + 1],
                in1=T1[:, 0:128], op0=ALU.mult, op1=ALU.add), R=[t_r, t_tab, t_const, t_par], W=[t_m1, t_r])
        TA1 = T1[:, 0:NC]
        TB1 = T3[:, 0:NC]
        for p in range(4):
            pg = 4 * r + p
            pb = pg % 2
            INTT = SQSG[:, pb, 0:NC].bitcast(I32)
            cosb, sinb = COS[:, pb, :], SIN[:, pb, :]
            for (TBL, off) in ((cosb, 0.25), (sinb, 0.0)):
                P.op("pool", lambda g, TBL=TBL, off=off: g.tensor_scalar(out=TBL, in0=CIDX, scalar1=FR[:, pg:pg + 1], scalar2=off, op0=ALU.mult, op1=ALU.add),
                     R=[t_tab], W=[t_trig[pb]])
                P.op("pool", lambda g, TBL=TBL: g.tensor_copy(out=INTT, in_=TBL), R=[t_trig[pb]], W=[t_gl])
                P.op("pool", lambda g, TBL=TBL: g.tensor_tensor(out=TBL, in0=TBL, in1=INTT, op=ALU.subtract), R=[t_gl], W=[t_trig[pb]])
                P.op("act", lambda a, TBL=TBL: a.activation(out=TBL, in_=TBL, func=AF.Sin, scale=TWO_PI), R=[t_trig[pb]], W=[t_trig[pb]])
            zr, zi = ZRE[:, p, :], ZIM[:, p, :]
            tg_ = t_trig[pb]

            def dv(o_, a_, b_, op_, R_, W_):
                P.op("dve", lambda v: v.tensor_tensor(out=o_, in0=a_, in1=b_, op=op_), R=R_, W=W_)
            dv(TA1, cosb, zr, ALU.mult, [tg_, t_z, t_r], [t_tmp, t_r])
            dv(TB1, sinb, zi, ALU.mult, [tg_, t_z, t_r], [t_tmp, t_r])
            dv(TA1, TA1, TB1, ALU.add, [t_tmp], [t_tmp, t_r])
            dv(TB1, sinb, zr, ALU.mult, [tg_, t_z, t_tmp], [t_tmp, t_r])
            dv(zi, cosb, zi, ALU.mult, [tg_, t_z], [t_z])
            dv(zi, zi, TB1, ALU.subtract, [t_z, t_tmp], [t_z])
            rho_b = RHO[:, pg:pg + 1].broadcast_to([128, NC])
            P.op("dve", lambda v: v.tensor_tensor_scan(out=TA1, data0=rho_b, data1=TA1, initial=0.0, op0=ALU.mult, op1=ALU.add),
                 R=[t_tmp, t_tab], W=[t_tmp, t_r])
            P.op("dve", lambda v: v.tensor_tensor_scan(out=zi, data0=rho_b, data1=zi, initial=0.0, op0=ALU.mult, op1=ALU.add),
                 R=[t_z, t_tab], W=[t_z])
            n1 = NC - 1
            dv(zr, cosb, TA1, ALU.mult, [tg_, t_tmp, t_z], [t_z])
            dv(TB1, sinb, zi, ALU.mult, [tg_, t_z, t_tmp], [t_tmp, t_r])
            dv(S1R[:, p, 1:NC], zr[:, 0:n1], TB1[:, 0:n1], ALU.subtract, [t_z, t_tmp], [t_s1])
            dv(zr, sinb, TA1, ALU.mult, [tg_, t_tmp, t_z], [t_z])
            dv(TB1, cosb, zi, ALU.mult, [tg_, t_z, t_tmp], [t_tmp, t_r])
            dv(S1I[:, p, 1:NC], zr[:, 0:n1], TB1[:, 0:n1], ALU.add, [tg_, t_z, t_tmp], [t_s1])
        for gl in range(8):
            pr = gl // 2
            mb = next_bank()
            P.op("pe", lambda t, mb=mb, gl=gl: t.matmul(bank[mb][:, :NC], lhsT=M1[:, gl, :], rhs=UF[:, gl, :], start=True, stop=False),
                 R=[t_m1, t_uf], W=[t_bank[mb]], inc=False)
            P.op("pe", lambda t, mb=mb, gl=gl, pr=pr: t.matmul(bank[mb][:, :NC], lhsT=QCZr[:, gl, :], rhs=S1R[:, pr, :], start=False, stop=False),
                 R=[t_m1, t_s1], W=[t_bank[mb]], inc=False)
            P.op("pe", lambda t, mb=mb, gl=gl, pr=pr: t.matmul(bank[mb][:, :NC], lhsT=QCZi[:, gl, :], rhs=S1I[:, pr, :], start=False, stop=True),
                 R=[t_m1, t_s1], W=[t_bank[mb]])
            P.op("act", lambda a, mb=mb, gl=gl: a.copy(out=YF[:, gl, :], in_=bank[mb][:, :NC]), R=[t_bank[mb], t_z], W=[t_z])
        for t8 in range(8):
            mb = next_bank()
            for gl in range(8):
                P.op("pe", lambda t, mb=mb, gl=gl, t8=t8: t.matmul(bank[mb][:, :NC], lhsT=SEL[:, t8 * 8 + gl, :], rhs=YF[:, gl, :],
                                                                  start=(gl == 0), stop=(gl == 7)),
                     R=[t_sel, t_z], W=[t_bank[mb]], inc=(gl == 7))
            yps = bank[mb][:, :NC]
            P.op("act", lambda a, yps=yps: a.activation(out=SQ[:, :NC], in_=yps, func=AF.Square), R=[t_bank[mb], t_gl], W=[t_gl])
            P.op("dve", lambda v: v.tensor_scalar(out=SQ[:, :NC], in0=SQ[:, :NC], scalar1=0.044715, scalar2=1.0, op0=ALU.mult, op1=ALU.add),
                 R=[t_gl], W=[t_gl])
            P.op("dve", lambda v, yps=yps: v.tensor_tensor(out=SQ[:, :NC], in0=SQ[:, :NC], in1=yps, op=ALU.mult), R=[t_bank[mb], t_gl], W=[t_gl])
            P.op("act", lambda a: a.activation(out=SG[:, :NC], in_=SQ[:, :NC], func=AF.Sigmoid, scale=1.5957691216), R=[t_gl], W=[t_gl])
            P.op("dve", lambda v, yps=yps, r=r, t8=t8: v.tensor_tensor(out=uD[:, r, t8, :], in0=SG[:, :NC], in1=yps, op=ALU.mult),
                 R=[t_bank[mb], t_gl, t_u[r]], W=[t_u[r], t_yg])
    SGALL = [UF, YF]
    t_sga = [Tok(), Tok()]
    NCQ = min(64, NC)
    for n in range(4):
        sg_, t_sg_ = SGALL[n % 2], t_sga[n % 2]
        for t8 in range(8):
            mb = next_bank()
            for c in range(4):
                P.op("pe", lambda t, mb=mb, c=c, n=n, t8=t8: t.matmul(bank[mb][:, :NC], lhsT=wglu[:, c, 128 * n:128 * n + 128], rhs=uD[:, c, t8, :],
                                                                    start=(c == 0), stop=(c == 3)),
                     R=[t_wglu] + t_u, W=[t_bank[mb]], inc=(c == 3))
            P.op("act", lambda a, mb=mb, sg_=sg_, t8=t8: a.activation(out=sg_[:, t8, :], in_=bank[mb][:, :NC], func=AF.Sigmoid),
                 R=[t_bank[mb], t_uf, t_z], W=[t_sg_])
        for q in range(NC // NCQ):
            csl = slice(NCQ * q, NCQ * q + NCQ)
            P.op("dve", lambda v, n=n, sg_=sg_, csl=csl, q=q: v.tensor_tensor(
                out=catT[:, 4 + n, 8 * NCQ * q:8 * NCQ * (q + 1)].rearrange("p (c s) -> p c s", s=8),
                in0=uD[:, n, :, csl].rearrange("p s c -> p c s"), in1=sg_[:, :, csl].rearrange("p s c -> p c s"), op=ALU.mult),
                R=[t_sg_] + t_u, W=[t_cat[4 + n]])

    if debug == "ssm":
        dbg2 = nc.dram_tensor("dbg", [128, 4, L], F32, kind="ExternalOutput").ap()
        barrier(P)
        WK.reset()
        stg = WK.get([128, 4, L], F32) if 16 * L <= 72 * 1024 else None
        t_stg = Tok()
        P.op("dve", lambda v: v.tensor_copy(out=stg[:], in_=catT[:, 4:8, :]), R=t_cat[4:8], W=[t_stg])
        ev = P.dma("sp", dbg2, stg[:], R=[t_stg])
        P.wait_ev("sp", ev)
        return P

    barrier(P)
    attn_g = nc.dram_tensor("attn_norm_g", [1, 512], F32, kind="ExternalInput").ap()
    ssm_g = nc.dram_tensor("ssm_norm_g", [1, 512], F32, kind="ExternalInput").ap()
    w_out = nc.dram_tensor("w_out", [1024, 1024], F32, kind="ExternalInput").ap()
    norm2_g = nc.dram_tensor("norm2_g", [1, D], F32, kind="ExternalInput").ap()
    final_g = nc.dram_tensor("final_norm_g", [1, D], F32, kind="ExternalInput").ap()
    XN.reset()
    WK.reset()
    TB = min(1024, L)
    NTB = L // TB
    TT = TB // 128
    ND = 8
    x1 = XN.get([128, TT, D], F32)
    h2T = XN.get([128, 8, TB], BF16)
    wg = [XN.get([128, 8, 128], BF16) for _ in range(2)]
    wup = [XN.get([128, 8, 128], BF16) for _ in range(2)]
    g2b = XN.get([128, D], BF16)
    wd2 = XN.get([128, D], BF16)
    gfb = XN.get([128, D], F32)
    actT = WK.get([128, NF, TB], BF16)
    wo = WK.get([128, 8, D], BF16)
    wd = [WK.get([128, D], BF16) for _ in range(2)] + [wd2]
    sq = [WK.get([128, 8, 128], BF16) for _ in range(2)]
    hb = [WK.get([128, D], BF16)] * 2
    slb = [WK.get([128, 512], BF16) for _ in range(2)]
    gcat = P.sb("gcat", [128, 8], F32)
    rst = P.sb("rst", [128, TT, 4], F32)
    t_g = Tok()
    t_g2 = Tok()
    P.dma("pool", g2b[:], norm2_g.partition_broadcast(128), W=[t_g2])
    P.dma("sp", gfb[:], final_g.partition_broadcast(128), W=[t_g])
    P.dma("sp", gcat[:, 0:4], attn_g.rearrange("o (c p) -> p (o c)", p=128), W=[t_g], allow_slow_non_contiguous=True)
    P.dma("sp", gcat[:, 4:8], ssm_g.rearrange("o (c p) -> p (o c)", p=128), W=[t_g], allow_slow_non_contiguous=True)
    t_x1 = [Tok() for _ in range(TT)]
    t_rst = [Tok() for _ in range(TT)]
    t_h2T, t_act = [Tok(), Tok()], Tok()
    t_wo = [Tok() for _ in range(8)]
    t_wd = [Tok(), Tok(), Tok()]
    t_sq = [Tok(), Tok()]
    t_hb = [Tok()] * 2
    t_wg = [Tok(), Tok()]
    t_wup = [Tok(), Tok()]
    t_sl = [Tok(), Tok()]
    cat_all = t_cat

    def rstd_inplace(col, scale, tk):
        P.op("dve", lambda v: v.tensor_scalar(out=col, in0=col, scalar1=scale, scalar2=EPS, op0=ALU.mult, op1=ALU.add), R=[tk], W=[tk])
        P.op("act", lambda a: a.activation(out=col, in_=col, func=AF.Sqrt), R=[tk], W=[tk])
        P.op("dve", lambda v: v.reciprocal(out=col, in_=col), R=[tk], W=[tk])

    for c in range(8):
        P.dma("pool", wo[:, c, :], w_out[128 * c:128 * c + 128, :], W=[t_wo[c]])
        P.op("dve", lambda v, c=c: v.tensor_scalar_mul(out=wo[:, c, :], in0=wo[:, c, :], scalar1=gcat[:, c:c + 1]),
             R=[t_wo[c], t_g], W=[t_wo[c]])
    fcount = 0
    dcount = 0
    tcount = 0
    for tb in range(NTB):

        def stA(il):
            i = tb * TT + il
            ts_ = slice(128 * i, 128 * i + 128)
            sb_ = (tcount + il) % 2
            P.dma("pool", x1[:, il, :], x[ts_, :], W=[t_x1[il]])
            P.op("pool", lambda g: g.tensor_tensor(out=sq[sb_][:], in0=catT[:, :, ts_], in1=catT[:, :, ts_], op=ALU.mult),
                 R=cat_all, W=[t_sq[sb_]])

        def stA2(il):
            sb_ = (tcount + il) % 2
            sbk = next_bank()
            for half in range(2):
                for c in range(4):
                    P.op("pe", lambda t, half=half, c=c: t.matmul(bank[sbk][:, half:half + 1], lhsT=sq[sb_][:, 4 * half + c, :], rhs=ones[:, 0:1],
                                                                  start=(c == 0), stop=(c == 3)),
                         R=[t_sq[sb_], t_const], W=[t_bank[sbk]], inc=(c == 3 and half == 1))
            P.op("dve", lambda v: v.tensor_copy(out=rst[:, il, 0:2], in_=bank[sbk][:, 0:2]), R=[t_bank[sbk]], W=[t_rst[il]])
            rstd_inplace(rst[:, il, 0:2], 1.0 / 512, t_rst[il])

        def stB(il):
            i = tb * TT + il
            ts_ = slice(128 * i, 128 * i + 128)
            for nh in range(2):
                ns_ = slice(512 * nh, 512 * nh + 512)
                pa, pb_ = next_bank(), next_bank()
                for c in range(4):
                    P.op("pe", lambda t, c=c: t.matmul(bank[pa][:], lhsT=catT[:, c, ts_], rhs=wo[:, c, ns_], start=(c == 0), stop=(c == 3)),
                         R=cat_all + [t_wo[c]], W=[t_bank[pa]], inc=(c == 3))
                for c in range(4, 8):
                    P.op("pe", lambda t, c=c: t.matmul(bank[pb_][:], lhsT=catT[:, c, ts_], rhs=wo[:, c, ns_], start=(c == 4), stop=(c == 7)),
                         R=cat_all + [t_wo[c]], W=[t_bank[pb_]], inc=(c == 7))
                P.op("dve", lambda v: v.scalar_tensor_tensor(out=x1[:, il, ns_], in0=bank[pa][:], scalar=rst[:, il, 0:1], in1=x1[:, il, ns_],
                                                              op0=ALU.mult, op1=ALU.add), R=[t_bank[pa], t_rst[il], t_x1[il]], W=[t_x1[il]])
                P.op("dve", lambda v: v.scalar_tensor_tensor(out=x1[:, il, ns_], in0=bank[pb_][:], scalar=rst[:, il, 1:2], in1=x1[:, il, ns_],
                                                              op0=ALU.mult, op1=ALU.add), R=[t_bank[pb_], t_rst[il], t_x1[il]], W=[t_x1[il]])

        def stC(il):
            hb_ = (tcount + il) % 2
            P.op("act", lambda a: a.activation(out=hb[hb_][:], in_=x1[:, il, :], func=AF.Square, accum_out=rst[:, il, 2:3]),
                 R=[t_x1[il]], W=[t_hb[hb_], t_rst[il]])
            rstd_inplace(rst[:, il, 2:3], 1.0 / D, t_rst[il])
            P.op("dve", lambda v: v.scalar_tensor_tensor(out=hb[hb_][:], in0=x1[:, il, :], scalar=rst[:, il, 2:3], in1=g2b[:], op0=ALU.mult, op1=ALU.mult),
                 R=[t_x1[il], t_rst[il], t_g2], W=[t_hb[hb_]])

        def stC2(il):
            hb_ = (tcount + il) % 2
            tbk = next_bank()
            pT = bank[tbk][:].bitcast(BF16)
            for j in range(8):
                P.op("pe", lambda t, j=j: t.transpose(out=pT[:, 128 * j:128 * j + 128], in_=hb[hb_][:, 128 * j:128 * j + 128], identity=identb[:]),
                     R=[t_hb[hb_], t_const], W=[t_bank[tbk]], inc=(j == 7))
            P.op("act", lambda a: a.copy(out=h2T[:, :, 128 * il:128 * il + 128], in_=pT.rearrange("p (j t) -> p j t", j=8)),
                 R=[t_bank[tbk]], W=[t_h2T[(128 * il) // 512 if TB > 512 else 0]])

        def op_stages(tiles):
            subs = (stA, stA2, stB, stC, stC2)
            lst = []
            n = len(tiles)
            for it in range(n + len(subs) - 1):
                grp = []
                for k in range(len(subs) - 1, -1, -1):
                    if 0 <= it - k < n:
                        grp.append((subs[k], tiles[it - k]))
                lst.append(lambda grp=grp: [f(il) for (f, il) in grp])
            return lst
        NTQ = TB // 512
        first_tiles = list(range(0, 4)) if NTQ > 1 else list(range(TT))
        rest_tiles = list(range(4, TT)) if NTQ > 1 else []
        for s_ in op_stages(first_tiles):
            s_()
        pending = op_stages(rest_tiles)
        for tq in range(NTQ):
            tq_ = slice(512 * tq, 512 * tq + 512)
            for f in range(NF):
                fb = fcount % 2
                fcount += 1
                P.dma("sp", wg[fb][:], wg_s[f].rearrange("p (c n) -> p c n", c=8), R=[t_wgs], W=[t_wg[fb]])
                P.dma("sp", wup[fb][:], wu_s[f].rearrange("p (c n) -> p c n", c=8), R=[t_wus], W=[t_wup[fb]])
                if tq == NTQ - 1 and f == NF - 3:
                    for q_ in range(2):
                        P.dma("sp", wd[(dcount + q_) % 3][:], wd_s[128 * q_:128 * q_ + 128, :], R=[t_wds], W=[t_wd[(dcount + q_) % 3]])
                gb_, ub_ = next_bank(), next_bank()
                for c in range(8):
                    P.op("pe", lambda t, gb_=gb_, c=c, fb=fb: t.matmul(bank[gb_][:], lhsT=wg[fb][:, c, :], rhs=h2T[:, c, tq_], start=(c == 0), stop=(c == 7)),
                         R=[t_wg[fb], t_h2T[tq]], W=[t_bank[gb_]], inc=(c == 7))
                for c in range(8):
                    P.op("pe", lambda t, ub_=ub_, c=c, fb=fb: t.matmul(bank[ub_][:], lhsT=wup[fb][:, c, :], rhs=h2T[:, c, tq_], start=(c == 0), stop=(c == 7)),
                         R=[t_wup[fb], t_h2T[tq]], W=[t_bank[ub_]], inc=(c == 7))
                sb2 = (2 * f + tq) % 2
                P.op("act", lambda a, gb_=gb_, sb2=sb2: a.activation(out=slb[sb2][:], in_=bank[gb_][:], func=AF.Silu), R=[t_bank[gb_]], W=[t_sl[sb2]])
                P.op("dve", lambda v, ub_=ub_, sb2=sb2, f=f: v.tensor_tensor(out=actT[:, f, tq_], in0=slb[sb2][:], in1=bank[ub_][:], op=ALU.mult),
                     R=[t_sl[sb2], t_bank[ub_]], W=[t_act])
                if pending:
                    pending.pop(0)()
            while pending:
                pending.pop(0)()
        tcount += TT
        steps = [(tg, f) for tg in range(TT // 4) for f in range(NF)]
        for si, (tg, f) in enumerate(steps):
            db_ = dcount % 3
            dcount += 1
            if si + 2 < len(steps):
                f2 = steps[si + 2][1]
                nb_ = (db_ + 2) % 3
                P.dma("sp", wd[nb_][:], wd_s[128 * f2:128 * f2 + 128, :], R=[t_wds], W=[t_wd[nb_]])
            for k in range(4):
                il = 4 * tg + k
                for nh in range(2):
                    bk = 2 * k + nh
                    P.op("pe", lambda t, bk=bk, f=f, il=il, nh=nh, db_=db_: t.matmul(bank[bk][:], lhsT=actT[:, f, 128 * il:128 * il + 128],
                                                                                  rhs=wd[db_][:, 512 * nh:512 * nh + 512], start=(f == 0), stop=(f == NF - 1)),
                         R=[t_act, t_wd[db_]], W=[t_bank[bk]], inc=(k == 3 and nh == 1))
            if f == NF - 1:
                for k in range(4):
                    il = 4 * tg + k
                    for nh in range(2):
                        bk = 2 * k + nh
                        ns_ = slice(512 * nh, 512 * nh + 512)
                        P.op("dve", lambda v, bk=bk, il=il, ns_=ns_: v.tensor_tensor(out=x1[:, il, ns_], in0=x1[:, il, ns_], in1=bank[bk][:], op=ALU.add),
                             R=[t_bank[bk], t_x1[il]], W=[t_x1[il]])
                for k in range(4):
                    il = 4 * tg + k
                    i = tb * TT + il
                    P.op("act", lambda a, il=il: a.activation(out=hb[0][:], in_=x1[:, il, :], func=AF.Square, accum_out=rst[:, il, 3:4]),
                         R=[t_x1[il]], W=[t_hb[0], t_rst[il]])
                    rstd_inplace(rst[:, il, 3:4], 1.0 / D, t_rst[il])
                    P.op("dve", lambda v, il=il: v.scalar_tensor_tensor(out=x1[:, il, :], in0=x1[:, il, :], scalar=rst[:, il, 3:4], in1=gfb[:], op0=ALU.mult, op1=ALU.mult),
                         R=[t_x1[il], t_rst[il], t_g], W=[t_x1[il]])
                    P.dma("pool", out[128 * i:128 * i + 128, :], x1[:, il, :], R=[t_x1[il]])
    barrier(P)
    return P


_CACHE = {}


def kernel(**inputs):
    L = 4096
    if "P" not in _CACHE:
        _CACHE["P"] = build(L=L)
    P = _CACHE["P"]
    f32 = lambda a: np.ascontiguousarray(np.asarray(a, dtype=np.float32))
    x = f32(inputs["x"])
    shared = {
        "norm1_g": f32(inputs["norm1_g"]).reshape(1, 1024),
        "w_in": f32(inputs["w_in"]).reshape(1024, 2048),
        "ident": np.eye(128, dtype=np.float32),
        "lambda_re": f32(inputs["lambda_re"]).reshape(32, 64),
        "lambda_im": f32(inputs["lambda_im"]).reshape(32, 64),
        "log_step": f32(inputs["log_step"]).reshape(1, 32),
        "b_re": f32(inputs["b_re"]).reshape(32, 64, 16),
        "b_im": f32(inputs["b_im"]).reshape(32, 64, 16),
        "c_re": f32(inputs["c_re"]).reshape(512, 64),
        "c_im": f32(inputs["c_im"]).reshape(512, 64),
        "d_skip": f32(inputs["d_skip"]).reshape(32, 16),
        "w_glu": f32(inputs["w_glu"]).reshape(512, 512),
        "attn_norm_g": f32(inputs["attn_norm_g"]).reshape(1, 512),
        "ssm_norm_g": f32(inputs["ssm_norm_g"]).reshape(1, 512),
        "w_out": f32(inputs["w_out"]).reshape(1024, 1024),
        "norm2_g": f32(inputs["norm2_g"]).reshape(1, 1024),
        "w_gate": f32(inputs["w_gate"]).reshape(1024, 2816),
        "w_up": f32(inputs["w_up"]).reshape(1024, 2816),
        "w_down": f32(inputs["w_down"]).reshape(2816, 1024),
        "final_norm_g": f32(inputs["final_norm_g"]).reshape(1, 1024),
    }
    in_maps = [dict(shared, x=x[b]) for b in range(8)]
    res = run_bass_kernel_spmd(P.nc, in_maps, core_ids=list(range(8)))
    return np.stack([np.asarray(r["out"], dtype=np.float32) for r in res.results], axis=0)
```
